# Optimizing a Trainium2 kernel written in Bass

```python
import math
import jax, jax.numpy as jnp
from jax import lax
import numpy as np

D_MODEL = 2048
BATCH = 2
SEQ = 4096
DEPTH = 1
DEC_BATCH = 32
DEC_SEQ = 8
PAST_LEN = 8192
PAGE_SIZE = 128

N_HEADS = 8
HEAD_DIM = 128
ATTN_WIDTH = N_HEADS * HEAD_DIM
MOBA_BLOCK = 256
MOBA_TOPK = 3
ROPE_THETA = 10000.0
POOL_WINDOWS = (2, 4, 8, 16)
POOL_WIDTH = D_MODEL // 2
POOL_GROUP = POOL_WIDTH // len(POOL_WINDOWS)
POOL_STATE = max(POOL_WINDOWS) - 1
N_BRANCHES = 2
IN_WIDTH = 3 * ATTN_WIDTH + POOL_WIDTH + N_BRANCHES * D_MODEL
FFN_HIDDEN = ((8 * D_MODEL + 3 * 256 - 1) // (3 * 256)) * 256
QUERY_BLOCK = 32
RMS_EPS = 1e-6

kernel_name = "moba_pool_gated_hybrid_step"


def rms_norm(x, g):
    xf = x.astype(jnp.float32)
    y = xf * lax.rsqrt(jnp.mean(xf * xf, axis=-1, keepdims=True) + RMS_EPS)
    return (y * g.astype(jnp.float32)).astype(x.dtype)


def rope(x, pos):
    half = HEAD_DIM // 2
    inv = ROPE_THETA ** (-jnp.arange(half, dtype=jnp.float32) / half)
    ang = pos.astype(jnp.float32)[:, None] * inv[None, :]
    cos = jnp.cos(ang)[None, :, None, :]
    sin = jnp.sin(ang)[None, :, None, :]
    xf = x.astype(jnp.float32)
    x1, x2 = xf[..., :half], xf[..., half:]
    return jnp.concatenate([x1 * cos - x2 * sin, x2 * cos + x1 * sin], axis=-1).astype(x.dtype)


def moba_attention(q, k_all, v_all, pos0):
    B, Sq, H, hd = q.shape
    Sk = k_all.shape[1]
    nb = Sk // MOBA_BLOCK
    kb = k_all.reshape(B, nb, MOBA_BLOCK, H, hd)
    vb = v_all.reshape(B, nb, MOBA_BLOCK, H, hd)
    kmean = jnp.mean(kb.astype(jnp.float32), axis=2)
    qc = math.gcd(Sq, QUERY_BLOCK)
    n_chunks = Sq // qc
    topk = min(MOBA_TOPK, nb)
    qch = q.reshape(B, n_chunks, qc, H, hd)
    scale = HEAD_DIM ** -0.5
    h_idx = jnp.arange(H)[None, :, None]
    blk = jnp.arange(nb)
    offs = jnp.arange(MOBA_BLOCK)

    def attend_chunk(ids):
        b, c = ids
        qq = qch[b, c]
        pos = pos0 + c * qc + jnp.arange(qc)
        own = pos // MOBA_BLOCK
        gate = jnp.einsum('qhd,nhd->qhn', qq.astype(jnp.float32), kmean[b])
        gate = jnp.where(blk[None, None, :] < own[:, None, None], gate, -jnp.inf)
        top_val, top_idx = lax.top_k(gate, topk)
        own_b = jnp.broadcast_to(own[:, None, None], (qc, H, 1))
        idx = jnp.concatenate([top_idx, own_b], axis=-1)
        valid = jnp.concatenate([jnp.isfinite(top_val), jnp.ones((qc, H, 1), bool)], axis=-1)
        ksel = kb[b][idx, :, h_idx, :]
        vsel = vb[b][idx, :, h_idx, :]
        s = jnp.einsum('qhd,qhjkd->qhjk', qq, ksel, preferred_element_type=jnp.float32) * scale
        kpos = idx[..., None] * MOBA_BLOCK + offs
        mask = valid[..., None] & (kpos <= pos[:, None, None, None])
        s = jnp.where(mask, s, -jnp.inf)
        p = jax.nn.softmax(s.reshape(qc, H, -1), axis=-1).reshape(s.shape)
        o = jnp.einsum('qhjk,qhjkd->qhd', p.astype(vsel.dtype), vsel,
                       preferred_element_type=jnp.float32)
        return o.astype(q.dtype)

    b_ids = jnp.repeat(jnp.arange(B), n_chunks)
    c_ids = jnp.tile(jnp.arange(n_chunks), B)
    out = lax.map(attend_chunk, (b_ids, c_ids))
    return out.reshape(B, Sq, H, hd)


def multiscale_pool(u_ext, n_pre, pos0, w_pool, ls_pool):
    B, S_ext, _ = u_ext.shape
    S = S_ext - n_pre
    csum = jnp.cumsum(u_ext.astype(jnp.float32), axis=1)
    csum = jnp.concatenate([jnp.zeros((B, 1, POOL_WIDTH), jnp.float32), csum], axis=1)
    i = n_pre + np.arange(S)
    abs_pos = pos0 + np.arange(S)
    u_new = u_ext[:, n_pre:].astype(jnp.float32)
    outs = []
    for g, w in enumerate(POOL_WINDOWS):
        sl = slice(g * POOL_GROUP, (g + 1) * POOL_GROUP)
        cg = csum[:, :, sl]
        start = np.maximum(i + 1 - w, 0)
        cnt = jnp.asarray(np.minimum(w, abs_pos + 1).astype(np.float32))
        mean = (cg[:, i + 1] - cg[:, start]) / cnt[None, :, None]
        outs.append(mean - u_new[:, :, sl])
    y = jnp.stack(outs, axis=2)
    y = jnp.einsum('bsgc,gcd->bsgd', y.astype(u_ext.dtype), w_pool,
                   preferred_element_type=jnp.float32)
    y = y.reshape(B, S, POOL_WIDTH) * ls_pool.astype(jnp.float32)
    return y.astype(u_ext.dtype)


def trunk_layer(x, c, pos0, past_k, past_v, past_u,
                w_ada, b_ada, g_norm1, g_norm2, w_in, g_qnorm, g_knorm, w_pool, ls_pool,
                w_o_attn, w_o_pool, w_out, w_ffn_in, w_ffn_out):
    B, S, _ = x.shape
    mod = (c @ w_ada + b_ada)[:, None, :]
    sh1, sc1, gt1, sh2, sc2, gt2 = jnp.split(mod, 6, axis=-1)

    h = rms_norm(x, g_norm1) * (1 + sc1) + sh1
    proj = h @ w_in
    q, k, v, u, glog = jnp.split(
        proj, [ATTN_WIDTH, 2 * ATTN_WIDTH, 3 * ATTN_WIDTH, 3 * ATTN_WIDTH + POOL_WIDTH], axis=-1)
    pos = pos0 + jnp.arange(S)
    q = rope(rms_norm(q.reshape(B, S, N_HEADS, HEAD_DIM), g_qnorm), pos)
    k = rope(rms_norm(k.reshape(B, S, N_HEADS, HEAD_DIM), g_knorm), pos)
    v = v.reshape(B, S, N_HEADS, HEAD_DIM)

    if past_k is None:
        k_all, v_all, u_ext, n_pre = k, v, u, 0
    else:
        k_all = jnp.concatenate([past_k.astype(k.dtype), k], axis=1)
        v_all = jnp.concatenate([past_v.astype(v.dtype), v], axis=1)
        u_ext = jnp.concatenate([past_u.astype(u.dtype), u], axis=1)
        n_pre = past_u.shape[1]
    pad = (-k_all.shape[1]) % MOBA_BLOCK
    k_pad = jnp.pad(k_all, ((0, 0), (0, pad), (0, 0), (0, 0)))
    v_pad = jnp.pad(v_all, ((0, 0), (0, pad), (0, 0), (0, 0)))

    attn = moba_attention(q, k_pad, v_pad, pos0).reshape(B, S, ATTN_WIDTH)
    pooled = multiscale_pool(u_ext, n_pre, pos0, w_pool, ls_pool)
    y_a = attn @ w_o_attn
    y_b = pooled @ w_o_pool
    g_a, g_b = jnp.split(jax.nn.sigmoid(glog), N_BRANCHES, axis=-1)
    mix = (g_a * y_a + g_b * y_b) @ w_out
    x = x + gt1 * mix

    h2 = rms_norm(x, g_norm2) * (1 + sc2) + sh2
    a, b = jnp.split(h2 @ w_ffn_in, 2, axis=-1)
    x = x + gt2 * ((jax.nn.silu(a) * b) @ w_ffn_out)
    return x, k, v, u_ext[:, -POOL_STATE:]


def setup_inputs(seed: int = 0) -> dict:
    key = jax.random.key(seed)
    ks = jax.random.split(key, 24)
    f32 = jnp.float32

    def nrm(k, shape, s):
        return jax.random.normal(k, shape, f32) * s

    n_pages = PAST_LEN // PAGE_SIZE
    n_used = DEC_BATCH * n_pages
    n_phys = n_used + n_used // 4
    page_table = jax.random.permutation(ks[0], n_phys)[:n_used].astype(jnp.int32).reshape(DEC_BATCH, n_pages)
    return {
        "x_prompt": nrm(ks[1], (BATCH, SEQ, D_MODEL), 1.0),
        "x_sample": nrm(ks[2], (DEC_BATCH, DEC_SEQ, D_MODEL), 1.0),
        "c_prompt": nrm(ks[3], (BATCH, D_MODEL), 1.0),
        "c_sample": nrm(ks[4], (DEC_BATCH, D_MODEL), 1.0),
        "cache_k": nrm(ks[5], (n_phys, PAGE_SIZE, N_HEADS, HEAD_DIM), 1.0),
        "cache_v": nrm(ks[6], (n_phys, PAGE_SIZE, N_HEADS, HEAD_DIM), 1.0),
        "state_pool": nrm(ks[7], (DEC_BATCH, POOL_STATE, POOL_WIDTH), 1.0),
        "page_table": page_table,
        "w_ada": nrm(ks[8], (D_MODEL, 6 * D_MODEL), 0.3 * D_MODEL ** -0.5),
        "b_ada": nrm(ks[9], (6 * D_MODEL,), 0.01),
        "g_norm1": 1.0 + nrm(ks[10], (D_MODEL,), 0.02),
        "g_norm2": 1.0 + nrm(ks[11], (D_MODEL,), 0.02),
        "w_in": nrm(ks[12], (D_MODEL, IN_WIDTH), D_MODEL ** -0.5),
        "g_qnorm": 1.0 + nrm(ks[13], (HEAD_DIM,), 0.02),
        "g_knorm": 1.0 + nrm(ks[14], (HEAD_DIM,), 0.02),
        "w_pool": nrm(ks[15], (len(POOL_WINDOWS), POOL_GROUP, POOL_GROUP), POOL_GROUP ** -0.5),
        "ls_pool": 1.0 + nrm(ks[16], (POOL_WIDTH,), 0.05),
        "w_o_attn": nrm(ks[17], (ATTN_WIDTH, D_MODEL), ATTN_WIDTH ** -0.5),
        "w_o_pool": nrm(ks[18], (POOL_WIDTH, D_MODEL), POOL_WIDTH ** -0.5),
        "w_out": nrm(ks[19], (D_MODEL, D_MODEL), D_MODEL ** -0.5),
        "w_ffn_in": nrm(ks[20], (D_MODEL, 2 * FFN_HIDDEN), D_MODEL ** -0.5),
        "w_ffn_out": nrm(ks[21], (FFN_HIDDEN, D_MODEL), FFN_HIDDEN ** -0.5),
    }


def reference(x_prompt, x_sample, c_prompt, c_sample, cache_k, cache_v, state_pool, page_table,
              w_ada, b_ada, g_norm1, g_norm2, w_in, g_qnorm, g_knorm, w_pool, ls_pool,
              w_o_attn, w_o_pool, w_out, w_ffn_in, w_ffn_out):
    past_len = page_table.shape[1] * cache_k.shape[1]
    dec_b = page_table.shape[0]
    past_k = cache_k[page_table].reshape(dec_b, past_len, N_HEADS, HEAD_DIM)
    past_v = cache_v[page_table].reshape(dec_b, past_len, N_HEADS, HEAD_DIM)

    y_p = x_prompt
    y_s = x_sample
    for _ in range(DEPTH):
        y_p, k_prompt, v_prompt, pool_prompt = trunk_layer(
            y_p, c_prompt, 0, None, None, None,
            w_ada, b_ada, g_norm1, g_norm2, w_in, g_qnorm, g_knorm, w_pool, ls_pool,
            w_o_attn, w_o_pool, w_out, w_ffn_in, w_ffn_out)
        y_s, k_sample, v_sample, pool_sample = trunk_layer(
            y_s, c_sample, past_len, past_k, past_v, state_pool,
            w_ada, b_ada, g_norm1, g_norm2, w_in, g_qnorm, g_knorm, w_pool, ls_pool,
            w_o_attn, w_o_pool, w_out, w_ffn_in, w_ffn_out)
    return (y_p, y_s, k_prompt, v_prompt, pool_prompt, k_sample, v_sample, pool_sample)
```

```python
import numpy as np
import ml_dtypes
from contextlib import ExitStack
import concourse.bass as bass
import concourse.mybir as mybir
from concourse.bass_utils import run_bass_kernel_spmd

F32 = mybir.dt.float32
BF16 = mybir.dt.bfloat16
I32 = mybir.dt.int32
AF = mybir.ActivationFunctionType
ALU = mybir.AluOpType
AX = mybir.AxisListType

D = 2048
KC = 16
NH = 8
HD = 128
AW = 1024
PW = 1024
FH = 5632
NPAGE = 64
NPHYS = 2560
EPS = 1e-6
BIG = 30000.0
NOWN = 1024
NTOK = 1184
SCOL = 1152
NQ = 1056
NCROW = 5

import os
PHASES = int(os.environ.get("KPH", "99"))


class _Op:
    __slots__ = ("eng", "fn", "r", "w", "dma", "key", "deps", "waits", "inc", "after", "bar")


class Prog:
    ENGS = ("sp", "act", "dve", "pool", "pe")

    def __init__(self, nc):
        self.nc = nc
        self.ops = []

    def add(self, eng, fn, r=(), w=(), dma=False, key=None, after=()):
        o = _Op()
        w = tuple(w) + tuple(h for h in r if h[:2] in ("mm", "tp") or h[:3] == "acc")
        o.eng, o.fn, o.r, o.w, o.dma, o.key = eng, fn, tuple(r), tuple(w), dma, key
        o.deps, o.waits, o.inc = set(), [], None
        o.after = tuple(after)
        o.bar = None
        self.ops.append(o)
        return len(self.ops) - 1

    def barrier(self):
        last = {}
        for i, o in enumerate(self.ops):
            last[o.eng] = i
        n0 = len(self.ops)
        for e_ in self.ENGS:
            i = self.add(e_, lambda e: e.nop(), after=list(last.values()))
            self.ops[i].bar = n0

    def finish(self):
        nc, ops = self.nc, self.ops
        last_w, readers = {}, {}
        for i, o in enumerate(ops):
            raw, other = set(), set()
            for h in o.r:
                if h in last_w:
                    raw.add(last_w[h])
            for h in o.w:
                if h in last_w:
                    other.add(last_w[h])
                other.update(readers.get(h, ()))
            for a in o.after:
                raw.add(a)
            raw.discard(i)
            other.discard(i)
            deps = set()
            for d in raw | other:
                od = ops[d]
                same = (od.eng == o.eng) and not od.dma and not o.dma
                if same and o.eng == "pe":
                    continue
                deps.add(d)
            o.deps = deps
            for h in o.r:
                readers.setdefault(h, []).append(i)
            for h in o.w:
                last_w[h] = i
                readers[h] = []
        needed = set()
        for o in ops:
            needed.update(o.deps)
        eng_sem = {e: nc.alloc_semaphore(name="es_" + e) for e in self.ENGS}
        eng_cnt = {e: 0 for e in self.ENGS}
        key_sem, key_cnt = {}, {}
        ev = {}
        for i, o in enumerate(ops):
            if o.dma:
                k = o.key
                if k not in key_sem:
                    key_sem[k] = nc.alloc_semaphore(name="ds_%d" % len(key_sem))
                    key_cnt[k] = 0
                key_cnt[k] += 16
                o.inc = (key_sem[k], 16)
                ev[i] = (key_sem[k], key_cnt[k])
            elif i in needed:
                eng_cnt[o.eng] += 1
                o.inc = (eng_sem[o.eng], 1)
                ev[i] = (eng_sem[o.eng], eng_cnt[o.eng])
        for o in ops:
            best = {}
            for d in o.deps:
                s, v = ev[d]
                if best.get(s, (None, 0))[1] < v:
                    best[s] = (s, v)
            o.waits = list(best.values())
        for o in ops:
            if o.bar is not None:
                have = dict((s_, v_) for s_, v_ in o.waits)
                cnts = {}
                for o2 in ops[:o.bar]:
                    if o2.dma:
                        cnts[o2.key] = cnts.get(o2.key, 0) + 16
                for k, v in cnts.items():
                    have[key_sem[k]] = max(have.get(key_sem[k], 0), v)
                o.waits = list(have.items())
        last_sp = [o for o in ops if o.eng == "sp"][-1]
        have = dict((s_, v_) for s_, v_ in last_sp.waits)
        for k in key_sem:
            have[key_sem[k]] = max(have.get(key_sem[k], 0), key_cnt[k])
        last_sp.waits = list(have.items())
        engobj = {"sp": nc.sync, "act": nc.scalar, "dve": nc.vector, "pool": nc.gpsimd, "pe": nc.tensor}
        self.nsem = len(key_sem) + 5
        print("SEMS", self.nsem, "eng counts", eng_cnt, "max dma key", max(key_cnt.values()), "nops", len(ops))

        def run(engname):
            def body(e):
                waited = {}
                for o in ops:
                    if o.eng != engname:
                        continue
                    for s, v in o.waits:
                        if waited.get(s, 0) < v:
                            e.wait_ge(s, v)
                            waited[s] = v
                    ins = o.fn(e)
                    if o.inc is not None:
                        ins.then_inc(o.inc[0], o.inc[1])
            return body

        with nc.Block() as block:
            block.sync(run("sp"))
            block.scalar(run("act"))
            block.vector(run("dve"))
            block.gpsimd(run("pool"))
            block.tensor(run("pe"))


def build_program():
    nc = bass.Bass("TRN2", target_bir_lowering=False)
    P = Prog(nc)

    def din(name, shape, dt=F32):
        return nc.dram_tensor(name, list(shape), dt, kind="ExternalInput").ap()

    def dout(name, shape, dt=F32):
        return nc.dram_tensor(name, list(shape), dt, kind="ExternalOutput").ap()

    def dscr(name, shape, dt):
        return nc.dram_tensor(name, list(shape), dt, kind="Internal").ap()

    cur = [None]

    def sb(name, shape, dt):
        if cur[0] is None:
            return nc.alloc_sbuf_tensor("s_" + name, list(shape), dt)
        return cur[0].enter_context(nc.sbuf_tensor("s_" + name, list(shape), dt))

    xkv = din("xkv", [4096, D])
    xhalo = din("xhalo", [128, D])
    xsm = din("xsm", [32, D])
    cc_d = din("cc", [NCROW, D])
    w_ada = din("w_ada", [D, 6 * D])
    b_ada = din("b_ada", [1, 6 * D])
    g_n1 = din("g_norm1", [16, 128])
    g_n2 = din("g_norm2", [16, 128])
    w_in = din("w_in", [D, 8192])
    g_q = din("g_qnorm", [1, 128])
    g_k = din("g_knorm", [1, 128])
    w_pool = din("w_pool", [4, 256, 256])
    ls_pool = din("ls_pool", [8, 128])
    w_oa = din("w_o_attn", [AW, D])
    w_op = din("w_o_pool", [PW, D])
    w_out = din("w_out", [D, D])
    w_f1 = din("w_ffn_in", [D, 2 * FH])
    w_f2 = din("w_ffn_out", [FH, D])
    state_d = din("state", [60, PW])
    rope_d = din("rope", [33, 128, 256])
    cident_bf = din("ident_bf", [128, 128], BF16)
    cident_f = din("ident_f", [128, 128])
    cband = din("band", [128, 4, 4, 128], BF16)
    cbandn = din("bandn", [32, 4, 32], BF16)
    cbando = din("bando", [60, 4, 32], BF16)
    cblockpos = din("blockpos", [128, 16])
    cownblk = din("ownblk", [128, 8])
    cE = din("Emat", [16, 16, 128], BF16)
    ctri = din("tri", [128, 256], BF16)
    cache_k = din("cache_k", [NPHYS * 128, AW])
    cache_v = din("cache_v", [NPHYS * 128, AW])
    pt_d = din("pt", [1, 256], I32)
    cmaskown = din("maskown", [64, 4, 32])
    crep = din("rep", [NCROW, 160])

    y_o = dout("y", [NQ, D])
    k_o = dout("ko", [NQ, AW])
    v_o = dout("vo", [NQ, AW])
    po_o = dout("po", [15, PW])
    pso_o = dout("pso", [4, 15, PW])

    kt_scr = dscr("kt_scr", [NH, 128, 4096], BF16)
    v_scr = dscr("v_scr", [32, 128, NH, 129], BF16)
    gt_scr = dscr("gt_scr", [NCROW, 2, D], F32)

    w_in_v = w_in.rearrange("(kc p) n -> p kc n", p=128)
    w_ada_v = w_ada.rearrange("(kc p) n -> p kc n", p=128)

    ident_bf = sb("ident_bf", [128, 128], BF16)
    ident_f = sb("ident_f", [128, 128], F32)
    AB = sb("AB", [128, 4, KC, NCROW], F32)
    hT = sb("hT", [128, KC, NTOK], BF16)
    KTs = sb("KTs", [128, NH, 32], BF16)
    ksum = sb("ksum", [128, NH, 16], F32)
    rep_sb = sb("rep_sb", [NCROW, 160], F32)
    Vs_bf = sb("Vs_bf", [32, AW], BF16)
    As_tab = sb("As_tab", [128, 2, 2, KC, 32], F32)
    evtmp = sb("evtmp", [128, 8, 32], F32)
    stMid = ExitStack()
    cur[0] = stMid
    QT = sb("QT", [128, NH, NQ], BF16)
    bigA = sb("bigA", [128, NH, NQ], BF16)
    KTslot = bigA[:, :, 0:1024]
    P2T = bigA
    stA = ExitStack()
    cur[0] = stA
    NWB = 2
    wbuf = [sb("wbuf%d" % i, [128, KC, 512], BF16) for i in range(NWB)]
    xt = [sb("xt%d" % i, [128, D], F32) for i in range(2)]
    xs = [sb("xs%d" % i, [128, D], BF16) for i in range(2)]
    ss = [sb("ss%d" % i, [128, 1], F32) for i in range(2)]
    rstd = [sb("rstd%d" % i, [128, 1], F32) for i in range(2)]
    modT = sb("modT", [128, 4, KC, NCROW], F32)
    gT = sb("gT", [128, 2, KC], F32)
    modtmp = [sb("modtmp%d" % i, [NCROW, 512], F32) for i in range(1)]
    bada = [sb("bada%d" % i, [NCROW, 512], F32) for i in range(1)]
    cc_sb = xt[1]
    ccT = sb("ccT", [128, KC, NCROW], BF16)
    gq_bc = sb("gq_bc", [128, 128], F32)
    gk_bc = sb("gk_bc", [128, 128], F32)
    rope_sb = [sb("rope%d" % i, [128, 256], F32) for i in range(2)]
    sq = [sb("sq%d" % i, [128, 512], F32) for i in range(2)]
    ss4 = [sb("ss4_%d" % i, [128, 4], F32) for i in range(2)]
    rs4 = [sb("rs4_%d" % i, [128, 4], F32) for i in range(2)]
    qn = [sb("qn%d" % i, [128, 512], F32) for i in range(2)]
    t1 = sq
    t2 = [sb("t2_%d" % i, [128, 512], F32) for i in range(2)]
    kf = [sb("kf%d" % i, [128, 512], F32) for i in range(2)]
    kbf = [sb("kbf%d" % i, [128, 512], BF16) for i in range(2)]
    vf = kf
    vaug = [sb("vaug%d" % i, [128, 4, 129], BF16) for i in range(2)]
    U_c = sb("U_c", [128, 10, 512], BF16)
    uf = kf
    state_bf = sb("state_bf", [60, PW], BF16)
    band = sb("band", [128, 4, 4, 128], BF16)
    bandn = sb("bandn", [32, 4, 32], BF16)
    bando = sb("bando", [60, 4, 32], BF16)
    poolT = sb("poolT", [128, 8, NQ], BF16)
    wp_sb = sb("wp_sb", [128, 4, 2, 256], BF16)
    lsT = sb("lsT", [128, 8], F32)
    g16 = kf[0][0:16, 0:256].rearrange("p (a d) -> p a d", d=128)
    ls8 = kf[1][0:8, 0:128]

    tp = [nc.alloc_psum_tensor("tp%d" % i, [128, 1024], BF16) for i in range(2)]
    mm = [nc.alloc_psum_tensor("mm%d" % i, [128, 512], F32) for i in range(4)]
    acc = [nc.alloc_psum_tensor("acc%d" % i, [128, 512], F32) for i in range(2)]

    SC = float(HD) ** -0.5
    ctr = {"mm": 0, "tp": 0, "w": 0, "x": 0, "s": 0, "acc": 0, "pt": 0, "rd": 0, "g": 0}

    def nxt(kind, n):
        v = ctr[kind]
        ctr[kind] = (v + 1) % n
        return v

    out_ops = []
    pending = []
    uid = [0]

    def dma(eng, out, in_, r, w, key):
        return P.add(eng, lambda e: e.dma_start(out=out, in_=in_), r=r, w=w, dma=True, key=key)

    def store(out, in_, r, key, w=None):
        if w is None:
            uid[0] += 1
            w = ["_st%d" % uid[0]]
        pending.append((out, in_, r, w, key))

    def flush():
        for (out, in_, r, w, key) in pending:
            out_ops.append(dma("sp", out, in_, r, w, key))
        del pending[:]

    def act(out, in_, func, r, w, bias=None, scale=None, accum=None):
        kw = {}
        if bias is not None:
            kw["bias"] = bias
        if scale is not None:
            kw["scale"] = scale
        if accum is not None:
            kw["accum_out"] = accum
        P.add("act", lambda e: e.activation(out=out, in_=in_, func=func, **kw), r=r, w=w)

    def tt(eng, out, in0, in1, op, r, w):
        P.add(eng, lambda e: e.tensor_tensor(out=out, in0=in0, in1=in1, op=op), r=r, w=w)

    def ts(eng, out, in0, s1, s2, op0, op1, r, w):
        if op1 is None:
            P.add(eng, lambda e: e.tensor_scalar(out=out, in0=in0, scalar1=s1, scalar2=None, op0=op0), r=r, w=w)
        else:
            P.add(eng, lambda e: e.tensor_scalar(out=out, in0=in0, scalar1=s1, scalar2=s2, op0=op0, op1=op1), r=r, w=w)

    def cp(eng, out, in_, r, w):
        if eng == "act":
            P.add("act", lambda e: e.copy(out=out, in_=in_), r=r, w=w)
        else:
            P.add(eng, lambda e: e.tensor_copy(out=out, in_=in_), r=r, w=w)

    def mmul(out, lhsT, rhs, start, stop, r, w, skip=False):
        if skip:
            P.add("pe", lambda e: e.matmul(out, lhsT, rhs, start=start, stop=stop, skip_group_check=True), r=r, w=w)
        else:
            P.add("pe", lambda e: e.matmul(out, lhsT, rhs, start=start, stop=stop), r=r, w=w)

    def transp(out, in_, ident, r, w):
        P.add("pe", lambda e: e.transpose(out, in_, ident), r=r, w=w)

    dma("sp", ident_bf[:], cident_bf, [], ["ident_bf"], "c0")
    dma("sp", ident_f[:], cident_f, [], ["ident_f"], "c1")
    dma("sp", band[:], cband, [], ["band"], "c2")
    dma("sp", bandn[:], cbandn, [], ["bandn"], "c3")
    dma("sp", bando[:], cbando, [], ["bando"], "c4")
    dma("sp", cc_sb[0:NCROW, :], cc_d, [], ["xt1"], "xt1")
    dma("sp", g16[:, 0, :], g_n1, [], ["kf0"], "c6")
    dma("sp", g16[:, 1, :], g_n2, ["kf0"], ["kf0"], "c7")
    dma("sp", gq_bc[:], g_q.partition_broadcast(128), [], ["gq_bc"], "c8")
    dma("sp", gk_bc[:], g_k.partition_broadcast(128), [], ["gk_bc"], "c9")
    dma("sp", ls8, ls_pool, [], ["kf1"], "c10")
    dma("pool", state_bf[:], state_d, [], ["state_bf"], "c11")
    dma("sp", rep_sb[:], crep, [], ["rep_sb"], "c12")
    dma("pool", wp_sb[:], w_pool.rearrange("g (cc p) d -> p g cc d", p=128), [], ["wp_sb"], "c13")

    b = nxt("mm", 4)
    transp(mm[b][:, 0:16], g16[:, 0, :], ident_f[0:16, 0:16], ["kf0", "ident_f"], ["mm%d" % b])
    transp(mm[b][:, 16:32], g16[:, 1, :], ident_f[0:16, 0:16], ["kf0", "ident_f"], ["mm%d" % b])
    transp(mm[b][:, 32:40], ls8, ident_f[0:8, 0:8], ["kf1", "ident_f"], ["mm%d" % b])
    cp("dve", gT[:, 0, :], mm[b][:, 0:16], ["mm%d" % b], ["gT"])
    cp("dve", gT[:, 1, :], mm[b][:, 16:32], ["mm%d" % b], ["gT"])
    cp("dve", lsT[:], mm[b][:, 32:40], ["mm%d" % b], ["lsT"])
    b = nxt("mm", 4)
    for kc in range(KC):
        transp(mm[b][:, kc * NCROW:(kc + 1) * NCROW], cc_sb[0:NCROW, kc * 128:(kc + 1) * 128],
               ident_f[0:NCROW, 0:NCROW], ["xt1", "ident_f"], ["mm%d" % b])
    cp("dve", ccT[:].rearrange("p k c -> p (k c)"), mm[b][:, 0:KC * NCROW], ["mm%d" % b], ["ccT"])

    for ci in range(24):
        wi = nxt("w", NWB)
        wk = "wbuf%d" % wi
        dma("pool", wbuf[wi][:], w_ada_v[:, :, ci * 512:(ci + 1) * 512], [], [wk], wk)
        bi = 0
        dma("sp", bada[bi][:], b_ada[:, ci * 512:(ci + 1) * 512].partition_broadcast(NCROW), [], ["bada%d" % bi],
            "bada%d" % bi)
        b = nxt("mm", 4)
        mk = "mm%d" % b
        for kc in range(KC):
            mmul(mm[b][0:NCROW, :], ccT[:, kc, :], wbuf[wi][:, kc, :], kc == 0, kc == KC - 1, ["ccT", wk], [mk])
        part = ci // 4
        if part in (2, 5):
            gi = 0 if part == 2 else 1
            col = (ci % 4) * 512
            tt("dve", modtmp[bi][:], mm[b][0:NCROW, :], bada[bi][:], ALU.add, [mk, "bada%d" % bi], ["modtmp%d" % bi])
            store(gt_scr[:, gi, col:col + 512], modtmp[bi][:], ["modtmp%d" % bi], "st_modtmp", w=["gt_scr%d_%d" % (gi, ci % 4)])
            flush()
        else:
            mi = {0: 0, 1: 1, 3: 2, 4: 3}[part]
            tt("dve", modtmp[bi][:], mm[b][0:NCROW, :], bada[bi][:], ALU.add, [mk, "bada%d" % bi], ["modtmp%d" % bi])
            b2 = nxt("mm", 4)
            for q4 in range(4):
                transp(mm[b2][:, q4 * NCROW:(q4 + 1) * NCROW], modtmp[bi][:, q4 * 128:(q4 + 1) * 128],
                       ident_f[0:NCROW, 0:NCROW], ["modtmp%d" % bi, "ident_f"], ["mm%d" % b2])
            k0 = (ci % 4) * 4
            cp("dve", modT[:, mi, k0:k0 + 4, :].rearrange("p k c -> p (k c)"), mm[b2][:, 0:4 * NCROW],
               ["mm%d" % b2], ["modT"])
    for s_ in range(2):
        ts("dve", AB[:, 2 * s_, :, :], modT[:, 2 * s_ + 1, :, :], 1.0, None, ALU.add, None, ["modT"], ["AB"])
        tt("dve", AB[:, 2 * s_, :, :], AB[:, 2 * s_, :, :], gT[:, s_, :].unsqueeze(2).to_broadcast([128, KC, NCROW]),
           ALU.mult, ["AB", "gT"], ["AB"])
        cp("dve", AB[:, 2 * s_ + 1, :, :], modT[:, 2 * s_, :, :], ["modT"], ["AB"])

    for w_ in range(2):
        for ab_ in range(2):
            for s_ in range(4):
                cp("dve", As_tab[:, ab_, w_, :, s_ * 8:(s_ + 1) * 8],
                   AB[:, 2 * w_ + ab_, :, 1 + s_:2 + s_].to_broadcast([128, KC, 8]), ["AB"], ["As_tab"])

    def build_hT(src_ap, M, col0, crow_of, which, rd=()):
        xi = nxt("x", 2)
        xk, xsk, ssk, rk = "xt%d" % xi, "xs%d" % xi, "ss%d" % xi, "rstd%d" % xi
        dma("sp", xt[xi][0:M, :], src_ap, list(rd), [xk], xk)
        flush()
        act(xs[xi][0:M, :], xt[xi][0:M, :], AF.Square, [xk], [xsk, ssk], accum=ss[xi][0:M, :])
        act(rstd[xi][0:M, :], ss[xi][0:M, :], AF.Ln, [ssk], [rk], bias=EPS, scale=1.0 / D)
        act(rstd[xi][0:M, :], rstd[xi][0:M, :], AF.Exp, [rk], [rk], scale=-0.5)
        ts("dve", xs[xi][0:M, :], xt[xi][0:M, :], rstd[xi][0:M, 0:1], None, ALU.mult, None, [xk, rk], [xsk])
        for half in range(2):
            ti = nxt("tp", 2)
            tk = "tp%d" % ti
            for k8 in range(8):
                kc = half * 8 + k8
                transp(tp[ti][:, k8 * 128:k8 * 128 + M], xs[xi][0:M, kc * 128:(kc + 1) * 128],
                       ident_bf[0:M, 0:M], [xsk, "ident_bf"], [tk])
            if M == 32:
                src = tp[ti][:, :].rearrange("p (k t) -> p k t", t=128)[:, :, 0:32]
                tt("dve", evtmp[:], src, As_tab[:, 0, which, half * 8:(half + 1) * 8, :], ALU.mult, [tk, "As_tab"],
                   ["evtmp"])
                tt("dve", hT[:, half * 8:(half + 1) * 8, col0:col0 + 32], evtmp[:],
                   As_tab[:, 1, which, half * 8:(half + 1) * 8, :], ALU.add, ["evtmp", "As_tab"], ["hT%d_%d" % (col0, kk) for kk in range(half * 8, half * 8 + 8)])
                continue
            for k8 in range(8):
                kc = half * 8 + k8
                for (c0, c1, crow) in crow_of:
                    act(hT[:, kc, col0 + c0:col0 + c1], tp[ti][:, k8 * 128 + c0:k8 * 128 + c1], AF.Identity,
                        [tk, "AB"], ["hT%d_%d" % (col0, kc)],
                        bias=AB[:, 2 * which + 1, kc, crow:crow + 1], scale=AB[:, 2 * which, kc, crow:crow + 1])

    SAMPLE_ROWS = [(8 * s_, 8 * s_ + 8, 1 + s_) for s_ in range(4)]
    PROMPT_ROWS = [(0, 128, 0)]

    def load_w(c):
        wi = nxt("w", NWB)
        wk = "wbuf%d" % wi
        dma("pool", wbuf[wi][:], w_in_v[:, :, c * 512:(c + 1) * 512], [], [wk], wk)
        return wi, wk

    def proj_tile(wi, wk, M, col0):
        b = nxt("mm", 4)
        mk = "mm%d" % b
        for kc in range(KC):
            mmul(mm[b][0:M, :], hT[:, kc, col0:col0 + M], wbuf[wi][:, kc, :], kc == 0, kc == KC - 1, ["hT%d_%d" % (col0, kc), wk], [mk])
        return b, mk

    def load_rope(idx, M):
        si = nxt("s", 2)
        rk = "rope%d" % si
        dma("sp", rope_sb[si][0:M, :], rope_d[idx, 0:M, :], [], [rk], rk)
        flush()
        return si, rk

    def qk_post(b, mk, M, g_bc, gk_name, ridx):
        si, rk = load_rope(ridx, M)
        s = si
        n = lambda base: "%s%d" % (base, s)
        act(sq[s][0:M, :], mm[b][0:M, :], AF.Square, [mk], [n("sq")])
        P.add("dve", lambda e: e.tensor_reduce(out=ss4[s][0:M, :], in_=sq[s][0:M, :].rearrange("p (h d) -> p h d", d=128),
                                               axis=AX.X, op=ALU.add), r=[n("sq")], w=[n("ss4")])
        act(rs4[s][0:M, :], ss4[s][0:M, :], AF.Ln, [n("ss4")], [n("rs4")], bias=EPS, scale=1.0 / HD)
        act(rs4[s][0:M, :], rs4[s][0:M, :], AF.Exp, [n("rs4")], [n("rs4")], scale=-0.5)
        qn3 = qn[s][0:M, :].rearrange("p (h d) -> p h d", d=128)
        tt("dve", qn3, mm[b][0:M, :].rearrange("p (h d) -> p h d", d=128),
           rs4[s][0:M, :].unsqueeze(2).to_broadcast([M, 4, 128]), ALU.mult, [mk, n("rs4")], [n("qn")])
        tt("pool", qn3, qn3, g_bc[0:M, :].unsqueeze(1).to_broadcast([M, 4, 128]), ALU.mult, [n("qn"), gk_name], [n("qn")])
        rp = rope_sb[si]
        t13 = t1[s][0:M, :].rearrange("p (h d) -> p h d", d=128)
        t23 = t2[s][0:M, :].rearrange("p (h d) -> p h d", d=128)
        tt("pool", t13, qn3, rp[0:M, 0:128].unsqueeze(1).to_broadcast([M, 4, 128]), ALU.mult, [n("qn"), rk], [n("sq")])
        tt("dve", t23[:, :, 0:64], qn3[:, :, 64:128], rp[0:M, 128:192].unsqueeze(1).to_broadcast([M, 4, 64]), ALU.mult,
           [n("qn"), rk], [n("t2_")])
        tt("pool", t23[:, :, 64:128], qn3[:, :, 0:64], rp[0:M, 192:256].unsqueeze(1).to_broadcast([M, 4, 64]), ALU.mult,
           [n("qn"), rk], [n("t2_")])
        tt("dve", kf[s][0:M, :], t1[s][0:M, :], t2[s][0:M, :], ALU.add, [n("sq"), n("t2_")], [n("kf")])
        cp("act", kbf[s][0:M, :], kf[s][0:M, :], [n("kf")], [n("kbf")])
        return s

    def qk_transposes(s, M, dst, dst_name, h0, col0):
        ti = nxt("tp", 2)
        tk = "tp%d" % ti
        for hh in range(4):
            transp(tp[ti][:, hh * 128:hh * 128 + M], kbf[s][0:M, hh * 128:(hh + 1) * 128], ident_bf[0:M, 0:M],
                   ["kbf%d" % s, "ident_bf"], [tk])
        cp("dve", dst[:, h0:h0 + 4, col0:col0 + M],
           tp[ti][:, 0:512].rearrange("p (h t) -> p h t", t=128)[:, :, 0:M], [tk], [dst_name])

    def k_chunks(slot, own):
        for c in (2, 3):
            wi, wk = load_w(c)
            h0 = (c - 2) * 4
            tiles = [(128, t_ * 128, slot * 8 + t_, t_) for t_ in range(8)]
            if own:
                tiles.append((32, SCOL, 32, 8))
            for (M, col0, ridx, t_) in tiles:
                b, mk = proj_tile(wi, wk, M, col0)
                s = qk_post(b, mk, M, gk_bc, "gk_bc", ridx)
                if own:
                    row0 = t_ * 128 if t_ < 8 else 1024
                    store(k_o[row0:row0 + M, (c - 2) * 512:(c - 1) * 512], kf[s][0:M, :], ["kf%d" % s], "st_kf%d" % s)
                if t_ < 8:
                    qk_transposes(s, M, KTslot, "bigA", h0, col0)
                else:
                    qk_transposes(s, M, KTs, "KTs", h0, 0)
        P.add("dve", lambda e: e.tensor_reduce(out=ksum[:, :, slot * 4:(slot + 1) * 4],
                                               in_=KTslot.rearrange("p h (n k) -> p h n k", k=256),
                                               axis=AX.X, op=ALU.add), r=["bigA"], w=["ksum"])
        store(kt_scr[:, :, slot * 1024:(slot + 1) * 1024].rearrange("h p t -> p h t"), KTslot, ["bigA"],
              "st_KTslot", w=["kt_scr%d" % slot])
        flush()

    def v_chunks(slot, own):
        for c in (4, 5):
            wi, wk = load_w(c)
            tiles = [(128, t_ * 128, t_) for t_ in range(8)]
            if own:
                tiles.append((32, SCOL, 8))
            for (M, col0, t_) in tiles:
                b, mk = proj_tile(wi, wk, M, col0)
                s = nxt("s", 2)
                if own:
                    cp("act", vf[s][0:M, :], mm[b][0:M, :], [mk], ["kf%d" % s])
                    row0 = t_ * 128 if t_ < 8 else 1024
                    store(v_o[row0:row0 + M, (c - 4) * 512:(c - 3) * 512], vf[s][0:M, :], ["kf%d" % s], "st_kf%d" % s)
                if t_ < 8:
                    P.add("pool", lambda e, s=s: e.memset(vaug[s][:, :, 128:129], 1.0), r=[], w=["vaug%d" % s])
                    cp("dve", vaug[s][:, :, 0:128], mm[b][:, :].rearrange("p (h d) -> p h d", d=128), [mk], ["vaug%d" % s])
                    store(v_scr[slot * 8 + t_, :, (c - 4) * 4:(c - 3) * 4, :], vaug[s][:], ["vaug%d" % s],
                          "st_vaug%d" % s, w=["v_scr%d_%d" % (slot * 8 + t_, c - 4)])
                    flush()
                else:
                    cp("dve", Vs_bf[:, (c - 4) * 512:(c - 3) * 512], mm[b][0:32, :], [mk], ["Vs_bf"])

    def pooling(cu):
        for gl in range(2):
            g = cu * 2 + gl
            for cc_ in range(2):
                c8 = g * 2 + cc_
                lc = (gl * 2 + cc_) * 128
                for grp in range(2):
                    b = nxt("mm", 4)
                    mk = "mm%d" % b
                    for tl in range(4):
                        t_ = grp * 4 + tl
                        prev = 8 if t_ == 0 else t_ - 1
                        kcur, kprev = (2, 3) if t_ == 0 else (0, 1)
                        mmul(mm[b][:, tl * 128:(tl + 1) * 128], U_c[:, t_, lc:lc + 128], band[:, kcur, g, :], True, False,
                             ["U_c", "band"], [mk])
                        mmul(mm[b][:, tl * 128:(tl + 1) * 128], U_c[:, prev, lc:lc + 128], band[:, kprev, g, :], False,
                             True, ["U_c", "band"], [mk])
                    cp("act", poolT[:, c8, grp * 512:(grp + 1) * 512], mm[b][:, :], [mk], ["poolT"])
                b = nxt("mm", 4)
                mk = "mm%d" % b
                mmul(mm[b][:, 0:32], U_c[0:32, 9, lc:lc + 128], bandn[:, g, :], True, False, ["U_c", "bandn"], [mk])
                mmul(mm[b][:, 0:32], state_bf[:, c8 * 128:(c8 + 1) * 128], bando[:, g, :], False, True,
                     ["state_bf", "bando"], [mk])
                cp("act", poolT[:, c8, 1024:1056], mm[b][:, 0:32], [mk], ["poolT"])

    def pool_map():
        for g in range(4):
            for dc in range(2):
                for (c0, n_) in ((0, 512), (512, 512), (1024, 32)):
                    b = nxt("mm", 4)
                    mk = "mm%d" % b
                    for cc_ in range(2):
                        mmul(mm[b][:, 0:n_], wp_sb[:, g, cc_, dc * 128:(dc + 1) * 128], poolT[:, g * 2 + cc_, c0:c0 + n_],
                             cc_ == 0, cc_ == 1, ["wp_sb", "poolT"], [mk])
                    act(P2T[:, g * 2 + dc, c0:c0 + n_], mm[b][:, 0:n_], AF.Copy, [mk, "lsT"], ["bigA"],
                        scale=lsT[:, g * 2 + dc:g * 2 + dc + 1])

    if PHASES >= 1:
        KSUB = int(os.environ.get("KSUB", "99"))
        for slot in (1, 2, 3):
            for t_ in range(8):
                r0 = slot * 1024 + t_ * 128
                build_hT(xkv[r0:r0 + 128, :], 128, t_ * 128, PROMPT_ROWS, 0)
            if KSUB >= 2:
                k_chunks(slot, False)
            if KSUB >= 3:
                v_chunks(slot, False)
            if KSUB < 4:
                break
    if PHASES >= 1 and KSUB >= 5:
        for t_ in range(8):
            build_hT(xkv[t_ * 128:(t_ + 1) * 128, :], 128, t_ * 128, PROMPT_ROWS, 0)
        KS5 = os.environ.get("KS5", "c")
        if KS5 >= "b":
            build_hT(xhalo, 128, 1024, PROMPT_ROWS, 0)
        if KS5 >= "c":
            build_hT(xsm, 32, SCOL, SAMPLE_ROWS, 0)
        if KSUB >= 6:
            k_chunks(0, True)
        if KSUB >= 7:
            v_chunks(0, True)
        for c in ((0, 1) if KSUB >= 8 else ()):
            wi, wk = load_w(c)
            tiles = [(128, t_ * 128, t_, t_) for t_ in range(8)] + [(32, SCOL, 32, 8)]
            for (M, col0, ridx, t_) in tiles:
                b, mk = proj_tile(wi, wk, M, col0)
                s = qk_post(b, mk, M, gq_bc, "gq_bc", ridx)
                qk_transposes(s, M, QT, "QT", c * 4, t_ * 128 if t_ < 8 else 1024)
        for c in ((6, 7) if KSUB >= 9 else ()):
            wi, wk = load_w(c)
            tiles = [(128, t_ * 128, t_) for t_ in range(8)] + [(128, 1024, 8), (32, SCOL, 9)]
            for (M, col0, t_) in tiles:
                b, mk = proj_tile(wi, wk, M, col0)
                cs = (c - 6) * 512
                cp("act", U_c[0:M, t_, :], mm[b][0:M, :], [mk], ["U_c"])
                if t_ == 7:
                    s = nxt("s", 2)
                    cp("dve", uf[s][:, :], mm[b][:, :], [mk], ["kf%d" % s])
                    store(po_o[:, cs:cs + 512], uf[s][113:128, :], ["kf%d" % s], "st_kf%d" % s)
                    flush()
                if t_ == 9:
                    s = nxt("s", 2)
                    cp("dve", uf[s][0:32, :], mm[b][0:32, :], [mk], ["kf%d" % s])
                    for q_ in range(4):
                        store(pso_o[q_, 7:15, cs:cs + 512], uf[s][q_ * 8:(q_ + 1) * 8, :], ["kf%d" % s],
                              "st_uf%d_%d" % (s, q_))
                    flush()
            pooling(c - 6)
        if KSUB >= 9:
            pool_map()
        for q_ in (range(4) if KSUB >= 10 else ()):
            store(pso_o[q_, 0:7, :], state_d[q_ * 15 + 8:q_ * 15 + 15, :], [], "st_state%d" % q_)
        flush()

    flush()
    P.barrier()
    stA.close()
    cur[0] = stMid
    attnT = sb("attnT", [128, NH, NQ], BF16)
    P.add("pool", lambda e: e.memset(attnT[:, :, 1024:NQ], 0.0), r=[], w=["attnT"])
    if PHASES >= 2:
        stB = ExitStack()
        cur[0] = stB
        attn_bf = sb("attn_bf", [128, 8, 1024], BF16)
        kmT = sb("kmT", [128, NH, 16], BF16)
        blockpos = sb("blockpos", [128, 16], F32)
        ownblk = sb("ownblk", [128, 8], F32)
        E_sb = sb("E_sb", [16, 16, 128], BF16)
        tri_sb = sb("tri_sb", [128, 256], BF16)
        vmask = [sb("vmask%d" % i, [128, 16], F32) for i in range(2)]
        negb = [sb("negb%d" % i, [128, 16], F32) for i in range(2)]
        gm = [sb("gm%d" % i, [128, NH, 16], F32) for i in range(2)]
        thr8 = [sb("thr8_%d" % i, [128, NH, 8], F32) for i in range(2)]
        sel = [sb("sel%d" % i, [128, NH, 16], F32) for i in range(2)]
        sbias = [sb("sbias%d" % i, [128, NH, 16], BF16) for i in range(2)]
        selT = sb("selT", [16, NH, 1024], BF16)
        KTh = [sb("KTh%d" % i, [128, 4096], BF16) for i in range(2)]
        Vh = [sb("Vh%d" % i, [128, 32, 129], BF16) for i in range(2)]
        PT = [sb("PT%d" % i, [128, 256], BF16) for i in range(3)]
        rden = [sb("rden%d" % i, [128, 2], F32) for i in range(2)]
        dma("sp", blockpos[:], cblockpos, [], ["blockpos"], "b0")
        dma("sp", ownblk[:], cownblk, [], ["ownblk"], "b1")
        dma("sp", E_sb[:], cE, [], ["E_sb"], "b2")
        dma("sp", tri_sb[:], ctri, [], ["tri_sb"], "b3")
        cp("dve", kmT[:], ksum[:], ["ksum"], ["kmT"])
        for qt in range(8):
            g = nxt("g", 2)
            G = lambda base: "%s%d" % (base, g)
            b = nxt("mm", 4)
            mk = "mm%d" % b
            for h in range(NH):
                mmul(mm[b][:, h * 16:(h + 1) * 16], QT[:, h, qt * 128:(qt + 1) * 128], kmT[:, h, :], True, True,
                     ["QT", "kmT"], [mk])
            ts("dve", vmask[g][:], blockpos[:], ownblk[:, qt:qt + 1], None, ALU.is_lt, None, ["blockpos", "ownblk"],
               [G("vmask")])
            ts("dve", negb[g][:], vmask[g][:], -1.0, 1e30, ALU.add, ALU.mult, [G("vmask")], [G("negb")])
            vm_b = vmask[g][:].unsqueeze(1).to_broadcast([128, NH, 16])
            tt("dve", gm[g][:], mm[b][:, 0:128].rearrange("p (h n) -> p h n", n=16), vm_b, ALU.mult, [mk, G("vmask")],
               [G("gm")])
            tt("dve", gm[g][:], gm[g][:], negb[g][:].unsqueeze(1).to_broadcast([128, NH, 16]), ALU.add,
               [G("gm"), G("negb")], [G("gm")])
            for h in range(NH):
                P.add("dve", lambda e, g=g, h=h: e.max(out=thr8[g][:, h, :], in_=gm[g][:, h, :]), r=[G("gm")],
                      w=[G("thr8_")])
            tt("dve", sel[g][:], gm[g][:], thr8[g][:, :, 2:3].to_broadcast([128, NH, 16]), ALU.is_ge,
               [G("gm"), G("thr8_")], [G("sel")])
            tt("dve", sel[g][:], sel[g][:], vm_b, ALU.mult, [G("sel"), G("vmask")], [G("sel")])
            ts("dve", sbias[g][:], sel[g][:], -1.0, BIG, ALU.add, ALU.mult, [G("sel")], [G("sbias")])
            ti = nxt("tp", 2)
            tk = "tp%d" % ti
            for h in range(NH):
                transp(tp[ti][0:16, h * 128:(h + 1) * 128], sbias[g][:, h, :], ident_bf[:, :], [G("sbias"), "ident_bf"],
                       [tk])
            cp("act", selT[:, :, qt * 128:(qt + 1) * 128], tp[ti][0:16, :].rearrange("p (h t) -> p h t", t=128), [tk],
               ["selT"])
        SC = float(HD) ** -0.5
        kt_handles = ["kt_scr%d" % i for i in range(4)]
        v_handles = ["v_scr%d_%d" % (t_, c_) for t_ in range(32) for c_ in range(2)]
        for h in range(NH):
            kb = h % 2
            dma("sp", KTh[kb][:], kt_scr[h], kt_handles, ["KTh%d" % kb], "KTh%d" % kb)
            dma("sp", Vh[kb][:], v_scr[:, :, h, :].rearrange("t p c -> p t c"), v_handles, ["Vh%d" % kb], "Vh%d" % kb)
            for qb in range(4):
                ab = nxt("acc", 2)
                ak = "acc%d" % ab
                accv = acc[ab][:, 0:258].rearrange("p (q c) -> p q c", c=129)
                tiles = [("d0", 2 * qb), ("d1", 2 * qb + 1)]
                tiles += [("p", kt) for lb in range(qb) for kt in (2 * lb, 2 * lb + 1)]
                tiles += [("p", kt) for kt in range(8, 32)]
                for idx, (kind, kt) in enumerate(tiles):
                    last = idx == len(tiles) - 1
                    b = nxt("mm", 4)
                    mk = "mm%d" % b
                    pi = nxt("pt", 3)
                    pk = "PT%d" % pi
                    if kind == "d1":
                        q0, nq = qb * 256 + 128, 128
                    else:
                        q0, nq = qb * 256, 256
                    mmul(mm[b][:, 0:nq], KTh[kb][:, kt * 128:(kt + 1) * 128], QT[:, h, q0:q0 + nq], True, False,
                         ["KTh%d" % kb, "QT"], [mk])
                    if kind == "p":
                        mmul(mm[b][:, 0:nq], E_sb[:, kt // 2, :], selT[:, h, q0:q0 + nq], False, True, ["E_sb", "selT"], [mk])
                    else:
                        mmul(mm[b][:, 0:nq], ident_bf[:, :], tri_sb[:, 0:nq], False, True, ["ident_bf", "tri_sb"], [mk])
                    act(PT[pi][:, 0:nq], mm[b][:, 0:nq], AF.Exp, [mk], [pk], scale=SC)
                    if kind == "d1":
                        mmul(accv[:, 1, :], PT[pi][:, 0:128], Vh[kb][:, kt, :], False, False, [pk, "Vh%d" % kb], [ak], skip=True)
                    else:
                        for qi in range(2):
                            mmul(accv[:, qi, :], PT[pi][:, qi * 128:(qi + 1) * 128], Vh[kb][:, kt, :],
                                 idx == 0 and qi == 0, last and qi == 1, [pk, "Vh%d" % kb], [ak], skip=True)
                ri = nxt("rd", 2)
                P.add("dve", lambda e, ri=ri, accv=accv: e.reciprocal(out=rden[ri][:], in_=accv[:, :, 128]), r=[ak],
                      w=["rden%d" % ri])
                for qi in range(2):
                    act(attn_bf[:, qb * 2 + qi, h * 128:(h + 1) * 128], accv[:, qi, 0:128], AF.Copy, [ak, "rden%d" % ri],
                        ["attn_bf"], scale=rden[ri][:, qi:qi + 1])
        for qt in range(8):
            ti = nxt("tp", 2)
            tk = "tp%d" % ti
            for h in range(NH):
                transp(tp[ti][:, h * 128:(h + 1) * 128], attn_bf[:, qt, h * 128:(h + 1) * 128], ident_bf[:, :],
                       ["attn_bf", "ident_bf"], [tk])
            cp("dve", attnT[:, :, qt * 128:(qt + 1) * 128], tp[ti][:, :].rearrange("p (h t) -> p h t", t=128), [tk],
               ["attnT"])
        flush()
        P.barrier()
        stB.close()

    if PHASES >= 5:
        stD = ExitStack()
        cur[0] = stD
        pt_sb = sb("pt_sb", [1, 256], I32)
        maskown = sb("maskown", [64, 4, 32], F32)
        S_raw = sb("S_raw", [64, 8192], F32)
        Pm = sb("Pm", [64, 8192 + 32], BF16)
        Kpg = [sb("Kpg%d" % i, [128, AW], BF16) for i in range(3)]
        Vpg = [sb("Vpg%d" % i, [128, AW], BF16) for i in range(3)]
        KTp = [sb("KTp%d" % i, [128, NH, 128], BF16) for i in range(2)]
        kpsum = sb("kpsum", [128, NH, 64], F32)
        ksum_s = sb("ksum_s", [128, NH, 32], F32)
        kmT_s = sb("kmT_s", [128, NH, 32], BF16)
        Qm = sb("Qm", [128, NH, 64], BF16)
        gate_s = sb("gate_s", [64, 32], F32)
        thr_s = sb("thr_s", [64, 8], F32)
        sbias_s = sb("sbias_s", [64, 32], F32)
        Sown = sb("Sown", [64, 32], F32)
        rowsum = sb("rowsum", [64, 33], F32)
        den = sb("den", [64, 1], F32)
        PTp = [sb("PTp%d" % i, [128, 64], BF16) for i in range(2)]
        PTo = sb("PTo", [32, 64], BF16)
        ctr["kp"] = 0
        ctr["vp"] = 0
        ctr["ktp"] = 0
        ctr["ptp"] = 0
        dma("sp", pt_sb[:], pt_d, [], ["pt_sb"], "d0")
        dma("sp", maskown[:], cmaskown, [], ["maskown"], "d1")
        accS = acc[0]
        first_pv = [True]

        ptb = sb("ptb", [128, 256], I32)
        ptf = sb("ptf", [128, 256], F32)
        iota_i = sb("iota_i", [128, 1], I32)
        iof = sb("iof", [128, 1], F32)
        idx_sb = sb("idx_sb", [128, 256], I32)
        NKF = 5
        Kf = [sb("Kf%d" % i, [128, AW], F32) for i in range(NKF)]
        dma("sp", ptb[:], pt_d.partition_broadcast(128), [], ["ptb"], "d2")
        P.add("pool", lambda e: e.iota(iota_i[:], [[0, 1]], base=0, channel_multiplier=1), r=[], w=["iota_i"])
        cp("dve", ptf[:], ptb[:], ["ptb"], ["ptf"])
        cp("dve", iof[:], iota_i[:], ["iota_i"], ["iof"])
        ts("dve", ptf[:], ptf[:], 128.0, None, ALU.mult, None, ["ptf"], ["ptf"])
        tt("dve", idx_sb[:], ptf[:], iof[:, 0:1].to_broadcast([128, 256]), ALU.add, ["ptf", "iof"], ["idx_sb"])
        ctr["kf"] = 0

        def page_load(dst, cache, idx, hname, cast_eng):
            fi = nxt("kf", NKF)
            fk = "Kf%d" % fi
            P.add("pool", lambda e: e.indirect_dma_start(
                out=Kf[fi][:], out_offset=None, in_=cache,
                in_offset=bass.IndirectOffsetOnAxis(ap=idx_sb[:, idx:idx + 1], axis=0)),
                r=["idx_sb"], w=[fk], dma=True, key=fk)
            cp(cast_eng, dst, Kf[fi][:], [fk], [hname])

        for sq_ in range(4):
            P.add("pool", lambda e: e.memset(Qm[:], 0.0), r=[], w=["Qm"])
            for h in range(NH):
                cp("dve", Qm[:, h, h * 8:(h + 1) * 8], QT[:, h, 1024 + sq_ * 8:1024 + sq_ * 8 + 8], ["Qm"], ["Qm"])
            for p in range(NPAGE):
                ki = nxt("kp", 3)
                page_load(Kpg[ki][:], cache_k, sq_ * 64 + p, "Kpg%d" % ki, "act")
                ti = nxt("tp", 2)
                tk = "tp%d" % ti
                for h in range(NH):
                    transp(tp[ti][:, h * 128:(h + 1) * 128], Kpg[ki][:, h * 128:(h + 1) * 128], ident_bf[:, :],
                           ["Kpg%d" % ki], [tk])
                kt_i = nxt("ktp", 2)
                cp("dve", KTp[kt_i][:].rearrange("p h k -> p (h k)"), tp[ti][:, :], [tk], ["KTp%d" % kt_i])
                P.add("dve", lambda e, kt_i=kt_i, p=p: e.tensor_reduce(out=kpsum[:, :, p], in_=KTp[kt_i][:], axis=AX.X,
                                                                        op=ALU.add), r=["KTp%d" % kt_i], w=["kpsum"])
                if p % 4 == 0:
                    b = nxt("mm", 4)
                    mk = "mm%d" % b
                for h in range(NH):
                    mmul(mm[b][0:64, (p % 4) * 128:(p % 4 + 1) * 128], Qm[:, h, :], KTp[kt_i][:, h, :], h == 0, h == NH - 1,
                         ["Qm", "KTp%d" % kt_i], [mk])
                if p % 4 == 3:
                    grp16 = p // 4
                    cp("act", S_raw[:, grp16 * 512:(grp16 + 1) * 512], mm[b][0:64, :], [mk], ["Sr%d" % grp16])
            tt("dve", ksum_s[:], kpsum[:].rearrange("p h (n two) -> p h n two", two=2)[:, :, :, 0],
               kpsum[:].rearrange("p h (n two) -> p h n two", two=2)[:, :, :, 1], ALU.add, ["kpsum"], ["ksum_s"])
            cp("dve", kmT_s[:], ksum_s[:], ["ksum_s"], ["kmT_s"])
            b = nxt("mm", 4)
            mk = "mm%d" % b
            for h in range(NH):
                mmul(mm[b][0:64, 0:32], Qm[:, h, :], kmT_s[:, h, :], h == 0, h == NH - 1, ["Qm", "kmT_s"], [mk])
            cp("dve", gate_s[:], mm[b][0:64, 0:32], [mk], ["gate_s"])
            P.add("dve", lambda e: e.max(out=thr_s[:], in_=gate_s[:]), r=["gate_s"], w=["thr_s"])
            ts("dve", sbias_s[:], gate_s[:], thr_s[:, 2:3], None, ALU.is_ge, None, ["gate_s", "thr_s"], ["sbias_s"])
            ts("dve", sbias_s[:], sbias_s[:], -1.0, BIG * SC, ALU.add, ALU.mult, ["sbias_s"], ["sbias_s"])
            b = nxt("mm", 4)
            mk = "mm%d" % b
            for h in range(NH):
                mmul(mm[b][0:64, 0:32], Qm[:, h, :], KTs[:, h, :], h == 0, h == NH - 1, ["Qm"], [mk])
            tt("dve", Sown[:], mm[b][0:64, 0:32], maskown[:, sq_, :], ALU.add, [mk, "maskown"], ["Sown"])
            for n in range(32):
                act(Pm[:, n * 256:(n + 1) * 256], S_raw[:, n * 256:(n + 1) * 256], AF.Exp,
                    ["Sr%d" % (n // 2), "sbias_s"], ["Pm%d" % n, "rowsum"], bias=sbias_s[:, n:n + 1], scale=SC,
                    accum=rowsum[:, n:n + 1])
            act(Pm[:, 8192:8224], Sown[:], AF.Exp, ["Sown"], ["Pm32", "rowsum"], scale=SC, accum=rowsum[:, 32:33])
            P.add("dve", lambda e: e.tensor_reduce(out=den[:], in_=rowsum[:], axis=AX.X, op=ALU.add), r=["rowsum"],
                  w=["den"])
            P.add("dve", lambda e: e.reciprocal(out=den[:], in_=den[:]), r=["den"], w=["den"])
            allP = ["Pm%d" % n for n in range(33)]
            ts("dve", Pm[:], Pm[:], den[:, 0:1], None, ALU.mult, None, allP + ["den"], allP)
            for p in range(NPAGE):
                vi = nxt("vp", 3)
                page_load(Vpg[vi][:], cache_v, sq_ * 64 + p, "Vpg%d" % vi, "act")
                ti = nxt("tp", 2)
                tk = "tp%d" % ti
                transp(tp[ti][:, 0:64], Pm[:, p * 128:(p + 1) * 128], ident_bf[0:64, 0:64], ["Pm%d" % (p // 2)], [tk])
                pi = nxt("ptp", 2)
                cp("dve", PTp[pi][:], tp[ti][:, 0:64], [tk], ["PTp%d" % pi])
                for h in range(NH):
                    mmul(accS[:, h * 32 + sq_ * 8:h * 32 + sq_ * 8 + 8], Vpg[vi][:, h * 128:(h + 1) * 128],
                         PTp[pi][:, h * 8:(h + 1) * 8], first_pv[0], False, ["Vpg%d" % vi, "PTp%d" % pi], ["acc0"], skip=True)
                    first_pv[0] = False
            ti = nxt("tp", 2)
            tk = "tp%d" % ti
            transp(tp[ti][0:32, 0:64], Pm[:, 8192:8224], ident_bf[0:64, 0:64], ["Pm32"], [tk])
            cp("dve", PTo[:], tp[ti][0:32, 0:64], [tk], ["PTo"])
            for h in range(NH):
                mmul(accS[:, h * 32 + sq_ * 8:h * 32 + sq_ * 8 + 8], Vs_bf[:, h * 128:(h + 1) * 128],
                     PTo[:, h * 8:(h + 1) * 8], False, (sq_ == 3 and h == NH - 1), ["PTo"], ["acc0"], skip=True)
        cp("dve", attnT[:, :, 1024:NQ], accS[:, 0:256].rearrange("p (h t) -> p h t", t=32), ["acc0"], ["attnT"])
        flush()
        P.barrier()
        stD.close()

    GROUPS = ((0, 512, 0), (512, 512, 512), (1024, 32, SCOL))
    TILES = [(t_ * 128, 128, t_ * 128) for t_ in range(8)] + [(1024, 32, SCOL)]
    mm6 = mm + acc
    mm6n = ["mm0", "mm1", "mm2", "mm3", "acc0", "acc1"]
    ctr["m6"] = 0
    if PHASES >= 3:
        stC = ExitStack()
        cur[0] = stC
        mixT = sb("mixT", [128, KC, NQ], BF16)
        stC1 = ExitStack()
        cur[0] = stC1
        wA = [sb("wA%d" % i, [128, 8, 256], BF16) for i in range(2)]
        wB = [sb("wB%d" % i, [128, 8, 256], BF16) for i in range(2)]
        wG = [sb("wG%d" % i, [128, KC, 256], BF16) for i in range(2)]
        wH = [sb("wH%d" % i, [128, KC, 256], BF16) for i in range(2)]
        sgA = [sb("sgA%d" % i, [128, 512], F32) for i in range(2)]
        sgB = [sb("sgB%d" % i, [128, 512], F32) for i in range(2)]
        w_oa_v = w_oa.rearrange("(kc p) n -> p kc n", p=128)
        w_op_v = w_op.rearrange("(kc p) n -> p kc n", p=128)
        for nc2 in range(8):
            wi = nc2 % 2
            c0_ = nc2 * 256
            dma("pool", wA[wi][:], w_oa_v[:, :, c0_:c0_ + 256], [], ["wA%d" % wi], "wA%d" % wi)
            dma("pool", wB[wi][:], w_op_v[:, :, c0_:c0_ + 256], [], ["wB%d" % wi], "wB%d" % wi)
            dma("pool", wG[wi][:], w_in_v[:, :, 4096 + c0_:4096 + c0_ + 256], [], ["wG%d" % wi], "wG%d" % wi)
            dma("pool", wH[wi][:], w_in_v[:, :, 6144 + c0_:6144 + c0_ + 256], [], ["wH%d" % wi], "wH%d" % wi)
            for nfl in range(2):
                nf = nc2 * 2 + nfl
                cs = nfl * 128
                for (t0, n_, hcol) in GROUPS:
                    bs = [nxt("m6", 6) for _ in range(4)]
                    for kc in range(8):
                        mmul(mm6[bs[0]][:, 0:n_], wA[wi][:, kc, cs:cs + 128], attnT[:, kc, t0:t0 + n_], kc == 0, kc == 7,
                             ["wA%d" % wi], [mm6n[bs[0]]])
                    for kc in range(8):
                        mmul(mm6[bs[1]][:, 0:n_], wB[wi][:, kc, cs:cs + 128], P2T[:, kc, t0:t0 + n_], kc == 0, kc == 7,
                             ["wB%d" % wi], [mm6n[bs[1]]])
                    for kc in range(KC):
                        mmul(mm6[bs[2]][:, 0:n_], wG[wi][:, kc, cs:cs + 128], hT[:, kc, hcol:hcol + n_], kc == 0,
                             kc == KC - 1, ["wG%d" % wi], [mm6n[bs[2]]])
                    for kc in range(KC):
                        mmul(mm6[bs[3]][:, 0:n_], wH[wi][:, kc, cs:cs + 128], hT[:, kc, hcol:hcol + n_], kc == 0,
                             kc == KC - 1, ["wH%d" % wi], [mm6n[bs[3]]])
                    si = nxt("s", 2)
                    act(sgA[si][:, 0:n_], mm6[bs[2]][:, 0:n_], AF.Sigmoid, [mm6n[bs[2]]], ["sgA%d" % si])
                    act(sgB[si][:, 0:n_], mm6[bs[3]][:, 0:n_], AF.Sigmoid, [mm6n[bs[3]]], ["sgB%d" % si])
                    tt("dve", sgA[si][:, 0:n_], sgA[si][:, 0:n_], mm6[bs[0]][:, 0:n_], ALU.mult,
                       ["sgA%d" % si, mm6n[bs[0]]], ["sgA%d" % si])
                    tt("dve", sgB[si][:, 0:n_], sgB[si][:, 0:n_], mm6[bs[1]][:, 0:n_], ALU.mult,
                       ["sgB%d" % si, mm6n[bs[1]]], ["sgB%d" % si])
                    tt("pool", mixT[:, nf, t0:t0 + n_], sgA[si][:, 0:n_], sgB[si][:, 0:n_], ALU.add,
                       ["sgA%d" % si, "sgB%d" % si], ["mixT%d" % nf])
        P.barrier()
        stC1.close()
        stC2 = ExitStack()
        cur[0] = stC2
        wO = [sb("wO%d" % i, [128, KC, 512], BF16) for i in range(2)]
        xr = [sb("xr%d" % i, [128, 512], F32) for i in range(2)]
        x1c = [sb("x1c%d" % i, [128, 512], F32) for i in range(2)]
        gtrow = [sb("gtrow%d" % i, [NCROW, 512], F32) for i in range(2)]
        gtbP = [sb("gtbP%d" % i, [128, 512], F32) for i in range(2)]
        gtbS = [sb("gtbS%d" % i, [32, 512], F32) for i in range(2)]
        w_out_v = w_out.rearrange("(kc p) n -> p kc n", p=128)
        gt_handles = ["gt_scr%d_%d" % (g_, c_) for g_ in range(2) for c_ in range(4)]
        for c4 in range(4):
            wi = c4 % 2
            dma("pool", wO[wi][:], w_out_v[:, :, c4 * 512:(c4 + 1) * 512], [], ["wO%d" % wi], "wO%d" % wi)
            dma("sp", gtrow[wi][:], gt_scr[:, 0, c4 * 512:(c4 + 1) * 512], [], ["gtrow%d" % wi], "gtrow%d" % wi)
            b = nxt("m6", 6)
            mmul(mm6[b][:, :], rep_sb[:, 0:128], gtrow[wi][:], True, True, ["gtrow%d" % wi], [mm6n[b]])
            cp("act", gtbP[wi][:], mm6[b][:, :], [mm6n[b]], ["gtbP%d" % wi])
            b = nxt("m6", 6)
            mmul(mm6[b][0:32, :], rep_sb[:, 128:160], gtrow[wi][:], True, True, ["gtrow%d" % wi], [mm6n[b]])
            cp("act", gtbS[wi][:], mm6[b][0:32, :], [mm6n[b]], ["gtbS%d" % wi])
            for ti_, (r0, M, hcol) in enumerate(TILES):
                xi = nxt("x", 2)
                src = xkv[r0:r0 + M, c4 * 512:(c4 + 1) * 512] if M == 128 else xsm[:, c4 * 512:(c4 + 1) * 512]
                dma("sp", xr[xi][0:M, :], src, [], ["xr%d" % xi], "xr%d" % xi)
                flush()
                b = nxt("m6", 6)
                for kc in range(KC):
                    mmul(mm6[b][0:M, :], mixT[:, kc, r0:r0 + M], wO[wi][:, kc, :], kc == 0, kc == KC - 1,
                         ["mixT%d" % kc, "wO%d" % wi], [mm6n[b]])
                gtb = gtbP[wi] if M == 128 else gtbS[wi]
                gk_ = ("gtbP%d" if M == 128 else "gtbS%d") % wi
                tt("dve", x1c[xi][0:M, :], mm6[b][0:M, :], gtb[0:M, :], ALU.mult, [mm6n[b], gk_], ["x1c%d" % xi])
                tt("pool", x1c[xi][0:M, :], x1c[xi][0:M, :], xr[xi][0:M, :], ALU.add, ["x1c%d" % xi, "xr%d" % xi],
                   ["x1c%d" % xi])
                store(y_o[r0:r0 + M, c4 * 512:(c4 + 1) * 512], x1c[xi][0:M, :], ["x1c%d" % xi], "st_x1c%d" % xi,
                      w=["y1_%d_%d" % (ti_, c4)])
        flush()
        P.barrier()
        stC2.close()
        stC.close()
    stMid.close()
    cur[0] = None

    if PHASES >= 4:
        stE = ExitStack()
        cur[0] = stE
        gTT = sb("gTT", [128, 44, NQ], BF16)
        stE1 = ExitStack()
        cur[0] = stE1
        xt = [sb("e_xt%d" % i, [128, D], F32) for i in range(2)]
        xs = [sb("e_xs%d" % i, [128, D], BF16) for i in range(2)]
        ss = [sb("e_ss%d" % i, [128, 1], F32) for i in range(2)]
        rstd = [sb("e_rstd%d" % i, [128, 1], F32) for i in range(2)]
        wa = [sb("wa%d" % i, [128, KC, 256], BF16) for i in range(2)]
        wb = [sb("wb%d" % i, [128, KC, 256], BF16) for i in range(2)]
        sil = [sb("sil%d" % i, [128, 512], F32) for i in range(2)]
        for ti_, (r0, M, hcol) in enumerate(TILES):
            build_hT(y_o[r0:r0 + M, :], M, hcol, PROMPT_ROWS if M == 128 else SAMPLE_ROWS, 1,
                     rd=["y1_%d_%d" % (ti_, c_) for c_ in range(4)])
        w_f1_v = w_f1.rearrange("(kc p) n -> p kc n", p=128)
        for ch in range(22):
            wi = ch % 2
            dma("pool", wa[wi][:], w_f1_v[:, :, ch * 256:(ch + 1) * 256], [], ["wa%d" % wi], "wa%d" % wi)
            dma("pool", wb[wi][:], w_f1_v[:, :, FH + ch * 256:FH + (ch + 1) * 256], [], ["wb%d" % wi], "wb%d" % wi)
            for hcl in range(2):
                hc = ch * 2 + hcl
                cs = hcl * 128
                for (t0, n_, hcol) in GROUPS:
                    b1 = nxt("m6", 6)
                    b2 = nxt("m6", 6)
                    tl = [hcol + 128 * i_ for i_ in range(4)] if n_ == 512 else [hcol]
                    for kc in range(KC):
                        mmul(mm6[b1][:, 0:n_], wa[wi][:, kc, cs:cs + 128], hT[:, kc, hcol:hcol + n_], kc == 0, kc == KC - 1,
                             ["wa%d" % wi] + ["hT%d_%d" % (c_, kc) for c_ in tl], [mm6n[b1]])
                    for kc in range(KC):
                        mmul(mm6[b2][:, 0:n_], wb[wi][:, kc, cs:cs + 128], hT[:, kc, hcol:hcol + n_], kc == 0, kc == KC - 1,
                             ["wb%d" % wi] + ["hT%d_%d" % (c_, kc) for c_ in tl], [mm6n[b2]])
                    si = nxt("s", 2)
                    act(sil[si][:, 0:n_], mm6[b1][:, 0:n_], AF.Silu, [mm6n[b1]], ["sil%d" % si])
                    tt("dve", gTT[:, hc, t0:t0 + n_], sil[si][:, 0:n_], mm6[b2][:, 0:n_], ALU.mult,
                       ["sil%d" % si, mm6n[b2]], ["gTT%d" % hc])
        P.barrier()
        stE1.close()
        stE2 = ExitStack()
        cur[0] = stE2
        wf2 = [sb("wf2_%d" % i, [128, 44, 256], BF16) for i in range(2)]
        xr = [sb("e_xr%d" % i, [128, 256], F32) for i in range(2)]
        x1c = [sb("e_x1c%d" % i, [128, 256], F32) for i in range(2)]
        gtrow = [sb("e_gtrow%d" % i, [NCROW, 256], F32) for i in range(2)]
        gtbP = [sb("e_gtbP%d" % i, [128, 256], F32) for i in range(2)]
        gtbS = [sb("e_gtbS%d" % i, [32, 256], F32) for i in range(2)]
        w_f2_v = w_f2.rearrange("(kc p) n -> p kc n", p=128)
        for n8 in range(8):
            wi = n8 % 2
            c0_ = n8 * 256
            dma("pool", wf2[wi][:], w_f2_v[:, :, c0_:c0_ + 256], [], ["wf2_%d" % wi], "wf2_%d" % wi)
            dma("sp", gtrow[wi][:], gt_scr[:, 1, c0_:c0_ + 256], [], ["e_gtrow%d" % wi], "e_gtrow%d" % wi)
            b = nxt("m6", 6)
            mmul(mm6[b][:, 0:256], rep_sb[:, 0:128], gtrow[wi][:], True, True, ["e_gtrow%d" % wi], [mm6n[b]])
            cp("act", gtbP[wi][:], mm6[b][:, 0:256], [mm6n[b]], ["e_gtbP%d" % wi])
            b = nxt("m6", 6)
            mmul(mm6[b][0:32, 0:256], rep_sb[:, 128:160], gtrow[wi][:], True, True, ["e_gtrow%d" % wi], [mm6n[b]])
            cp("act", gtbS[wi][:], mm6[b][0:32, 0:256], [mm6n[b]], ["e_gtbS%d" % wi])
            for ti_, (r0, M, hcol) in enumerate(TILES):
                xi = nxt("x", 2)
                dma("sp", xr[xi][0:M, :], y_o[r0:r0 + M, c0_:c0_ + 256], ["y1_%d_%d" % (ti_, n8 // 2)], ["e_xr%d" % xi],
                    "e_xr%d" % xi)
                flush()
                b = nxt("m6", 6)
                for kc in range(44):
                    mmul(mm6[b][0:M, 0:256], gTT[:, kc, r0:r0 + M], wf2[wi][:, kc, :], kc == 0, kc == 43,
                         ["gTT%d" % kc, "wf2_%d" % wi], [mm6n[b]])
                gtb = gtbP[wi] if M == 128 else gtbS[wi]
                gk_ = ("e_gtbP%d" if M == 128 else "e_gtbS%d") % wi
                tt("dve", x1c[xi][0:M, :], mm6[b][0:M, 0:256], gtb[0:M, :], ALU.mult, [mm6n[b], gk_], ["e_x1c%d" % xi])
                tt("pool", x1c[xi][0:M, :], x1c[xi][0:M, :], xr[xi][0:M, :], ALU.add, ["e_x1c%d" % xi, "e_xr%d" % xi],
                   ["e_x1c%d" % xi])
                store(y_o[r0:r0 + M, c0_:c0_ + 256], x1c[xi][0:M, :], ["e_x1c%d" % xi], "st_e_x1c%d" % xi)
        flush()
        P.barrier()
        stE2.close()
        stE.close()

    flush()
    P.add("sp", lambda e: e.nop(), r=[], w=[], after=list(out_ops))
    P.finish()
    return nc


def _bf(a):
    return np.ascontiguousarray(a).astype(ml_dtypes.bfloat16)


def _rope_table(pos):
    half = HD // 2
    inv = (10000.0 ** (-np.arange(half, dtype=np.float32) / half)).astype(np.float32)
    ang = pos.astype(np.float32)[:, None] * inv[None, :]
    cos = np.cos(ang).astype(np.float32)
    sin = np.sin(ang).astype(np.float32)
    return np.concatenate([cos, cos, -sin, sin], axis=1).astype(np.float32)


def _band_mats(first):
    W = (2, 4, 8, 16)
    out = np.zeros((128, 4, 4, 128), np.float32)
    for g, w in enumerate(W):
        for t in range(128):
            for d in range(w):
                tp_ = t - d
                if tp_ >= 0:
                    out[tp_, 0, g, t] += 1.0 / w
                else:
                    out[128 + tp_, 1, g, t] += 1.0 / w
            out[t, 0, g, t] -= 1.0
    out[:, 2] = out[:, 0]
    out[:, 3] = out[:, 1]
    if first:
        out[:, 3] = 0.0
        for g, w in enumerate(W):
            for t in range(min(w - 1, 128)):
                cnt = t + 1
                out[:, 2, g, t] = 0.0
                out[0:t + 1, 2, g, t] = 1.0 / cnt
                out[t, 2, g, t] -= 1.0
    return out


def _band_sample():
    W = (2, 4, 8, 16)
    bn = np.zeros((32, 4, 32), np.float32)
    bo = np.zeros((60, 4, 32), np.float32)
    for g, w in enumerate(W):
        for s in range(4):
            for t in range(8):
                i = 15 + t
                for d in range(w):
                    e = i - d
                    if e >= 15:
                        bn[s * 8 + (e - 15), g, s * 8 + t] += 1.0 / w
                    else:
                        bo[s * 15 + e, g, s * 8 + t] += 1.0 / w
                bn[s * 8 + t, g, s * 8 + t] -= 1.0
    return bn, bo


_NC_CACHE = {}


def kernel(x_prompt, x_sample, c_prompt, c_sample, cache_k, cache_v, state_pool, page_table,
           w_ada, b_ada, g_norm1, g_norm2, w_in, g_qnorm, g_knorm, w_pool, ls_pool,
           w_o_attn, w_o_pool, w_out, w_ffn_in, w_ffn_out):
    f = lambda a: np.ascontiguousarray(np.asarray(a, dtype=np.float32))
    x_prompt, x_sample, c_prompt, c_sample = f(x_prompt), f(x_sample), f(c_prompt), f(c_sample)
    state_pool = f(state_pool)
    if "nc" not in _NC_CACHE:
        _NC_CACHE["nc"] = build_program()
    nc = _NC_CACHE["nc"]

    ident = np.eye(128, dtype=np.float32)
    bn, bo = _band_sample()
    rep = np.zeros((NCROW, 160), np.float32)
    rep[0, 0:128] = 1.0
    for s in range(4):
        rep[1 + s, 128 + s * 8:128 + s * 8 + 8] = 1.0
    Emat = np.zeros((16, 16, 128), np.float32)
    for n_ in range(16):
        Emat[n_, n_, :] = 1.0
    tri = np.zeros((128, 256), np.float32)
    for k_ in range(128):
        tri[k_, 0:k_] = -BIG
    maskown = np.full((64, 4, 32), -BIG, np.float32)
    for s_ in range(4):
        for h_ in range(8):
            for q_ in range(8):
                maskown[h_ * 8 + q_, s_, s_ * 8:s_ * 8 + q_ + 1] = 0.0
    ck = np.asarray(cache_k, dtype=np.float32).reshape(NPHYS * 128, AW)
    cv = np.asarray(cache_v, dtype=np.float32).reshape(NPHYS * 128, AW)
    shared = {
        "cache_k": ck, "cache_v": cv, "maskown": maskown,
        "Emat": _bf(Emat), "tri": _bf(tri),
        "w_ada": f(w_ada), "b_ada": f(b_ada).reshape(1, -1), "g_norm1": f(g_norm1).reshape(16, 128),
        "g_norm2": f(g_norm2).reshape(16, 128), "w_in": f(w_in), "g_qnorm": f(g_qnorm).reshape(1, 128),
        "g_knorm": f(g_knorm).reshape(1, 128), "w_pool": f(w_pool), "ls_pool": f(ls_pool).reshape(8, 128),
        "w_o_attn": f(w_o_attn), "w_o_pool": f(w_o_pool), "w_out": f(w_out), "w_ffn_in": f(w_ffn_in),
        "w_ffn_out": f(w_ffn_out), "ident_bf": _bf(ident), "ident_f": ident, "bandn": _bf(bn), "bando": _bf(bo),
        "rep": rep,
    }
    in_maps = []
    for c in range(8):
        b, j = c // 4, c % 4
        order = [j] + [q for q in range(4) if q != j]
        xb = x_prompt[b]
        xkv = np.concatenate([xb[q * 1024:(q + 1) * 1024] for q in order], axis=0)
        xhalo = xb[j * 1024 - 128:j * 1024] if j > 0 else np.zeros((128, D), np.float32)
        pos = np.concatenate([np.arange(q * 1024, (q + 1) * 1024) for q in order])
        rope = np.zeros((33, 128, 256), np.float32)
        rope[0:32] = _rope_table(pos).reshape(32, 128, 256)
        rope[32, 0:32] = _rope_table(np.tile(8192 + np.arange(8), 4))
        blockpos = np.array([4 * q + i for q in order for i in range(4)], np.float32)
        ownblk = np.array([4 * j + qt // 2 for qt in range(8)], np.float32)
        m = dict(shared)
        m["pt"] = np.ascontiguousarray(np.asarray(page_table, dtype=np.int32)[4 * c:4 * c + 4].reshape(1, 256))
        m.update({
            "xkv": np.ascontiguousarray(xkv), "xhalo": np.ascontiguousarray(xhalo),
            "xsm": np.ascontiguousarray(x_sample[4 * c:4 * c + 4].reshape(32, D)),
            "cc": np.ascontiguousarray(np.concatenate([c_prompt[b:b + 1], c_sample[4 * c:4 * c + 4]], axis=0)),
            "state": np.ascontiguousarray(state_pool[4 * c:4 * c + 4].reshape(60, PW)),
            "rope": rope, "band": _bf(_band_mats(j == 0)),
            "blockpos": np.ascontiguousarray(np.broadcast_to(blockpos[None, :], (128, 16))),
            "ownblk": np.ascontiguousarray(np.broadcast_to(ownblk[None, :], (128, 8))),
        })
        in_maps.append(m)

    res = run_bass_kernel_spmd(nc, in_maps, core_ids=list(range(8)))
    R = res.results
    y_p = np.zeros((2, 4096, D), np.float32)
    k_p = np.zeros((2, 4096, NH, HD), np.float32)
    v_p = np.zeros((2, 4096, NH, HD), np.float32)
    pool_p = np.zeros((2, 15, PW), np.float32)
    y_s = np.zeros((32, 8, D), np.float32)
    k_s = np.zeros((32, 8, NH, HD), np.float32)
    v_s = np.zeros((32, 8, NH, HD), np.float32)
    pool_s = np.zeros((32, 15, PW), np.float32)
    for c in range(8):
        b, j = c // 4, c % 4
        r = R[c]
        sl = slice(j * 1024, (j + 1) * 1024)
        y_p[b, sl] = r["y"][0:1024]
        k_p[b, sl] = r["ko"][0:1024].reshape(1024, NH, HD)
        v_p[b, sl] = r["vo"][0:1024].reshape(1024, NH, HD)
        if j == 3:
            pool_p[b] = r["po"]
        y_s[4 * c:4 * c + 4] = r["y"][1024:1056].reshape(4, 8, D)
        k_s[4 * c:4 * c + 4] = r["ko"][1024:1056].reshape(4, 8, NH, HD)
        v_s[4 * c:4 * c + 4] = r["vo"][1024:1056].reshape(4, 8, NH, HD)
        pool_s[4 * c:4 * c + 4] = r["pso"]
    return (y_p, y_s, k_p, v_p, pool_p, k_s, v_s, pool_s)
```
